# Optimizing a Trainium2 kernel written in Bass

```python
import math
import jax, jax.numpy as jnp
from jax import lax
import numpy as np

D_MODEL = 1024
BATCH = 2
SEQ = 16384
DEPTH = 4

D_MIX = D_MODEL
HEAD_DIM = 64
DIFF_HEADS = D_MIX // 256
DIFF_QK_DIM = HEAD_DIM
DIFF_V_DIM = 2 * HEAD_DIM
DSA_HEADS = D_MIX // 256
DSA_DIM = HEAD_DIM
IDX_HEADS = D_MIX // 128
IDX_DIM = HEAD_DIM
FOX_HEADS = D_MIX // 256
FOX_DIM = HEAD_DIM
DIFF_WIDTH = DIFF_HEADS * DIFF_V_DIM
DSA_WIDTH = DSA_HEADS * DSA_DIM
FOX_WIDTH = FOX_HEADS * FOX_DIM
DSA_TOPK_MAX = 256
Q_BLOCK = 128
ROPE_THETA = 10000.0
LN_EPS = 1e-5
SUBLN_EPS = 1e-5
NEG_INF = -1e30
DEEPNORM_ALPHA = (2 * DEPTH) ** 0.25
DEEPNORM_BETA = (8 * DEPTH) ** -0.25
IDX_WEIGHT_SCALE = (IDX_HEADS * IDX_DIM) ** -0.5

IN_SPLITS = (
    ('diff_q', DIFF_HEADS * 2 * DIFF_QK_DIM),
    ('diff_k', DIFF_HEADS * 2 * DIFF_QK_DIM),
    ('diff_v', DIFF_HEADS * DIFF_V_DIM),
    ('dsa_q', DSA_HEADS * DSA_DIM),
    ('dsa_k', DSA_DIM),
    ('dsa_v', DSA_DIM),
    ('idx_q', IDX_HEADS * IDX_DIM),
    ('idx_k', IDX_DIM),
    ('idx_w', IDX_HEADS),
    ('fox_q', FOX_HEADS * FOX_DIM),
    ('fox_k', FOX_HEADS * FOX_DIM),
    ('fox_v', FOX_HEADS * FOX_DIM),
    ('fox_f', FOX_HEADS),
    ('gate', D_MIX),
)
VALUE_COLS = ('diff_v', 'dsa_v', 'fox_v')
IN_WIDTH = sum(n for _, n in IN_SPLITS)

kernel_name = 'hymba_style_diff_dsa_fox_deepnorm'


def split_columns(h):
    out, off = [], 0
    for _, n in IN_SPLITS:
        out.append(h[..., off:off + n])
        off += n
    return out


def rope(x, pos):
    d = x.shape[-1]
    inv = ROPE_THETA ** (-jnp.arange(0, d, 2, dtype=jnp.float32) / d)
    ang = pos.astype(jnp.float32)[:, None] * inv[None, :]
    shp = (x.shape[1],) + (1,) * (x.ndim - 3) + (d // 2,)
    cos = jnp.cos(ang).reshape(shp).astype(x.dtype)
    sin = jnp.sin(ang).reshape(shp).astype(x.dtype)
    x1, x2 = x[..., : d // 2], x[..., d // 2:]
    return jnp.concatenate([x1 * cos - x2 * sin, x2 * cos + x1 * sin], axis=-1)


def to_blocks(a):
    B, S = a.shape[:2]
    return jnp.moveaxis(a.reshape((B, S // Q_BLOCK, Q_BLOCK) + a.shape[2:]), 1, 0)


def from_blocks(a):
    nb, B, qb = a.shape[:3]
    return jnp.moveaxis(a, 0, 1).reshape((B, nb * qb) + a.shape[3:])


def layer_norm(x, g, b):
    xf = x.astype(jnp.float32)
    mu = jnp.mean(xf, axis=-1, keepdims=True)
    var = jnp.mean(jnp.square(xf - mu), axis=-1, keepdims=True)
    y = (xf - mu) * lax.rsqrt(var + LN_EPS) * g.astype(jnp.float32) + b.astype(jnp.float32)
    return y.astype(x.dtype)


def diff_attention(q, k, v, lam, pos):
    scale = DIFF_QK_DIM ** -0.5
    S = q.shape[1]

    def block(args):
        qb, t = args
        s = jnp.einsum('bqhmd,bkhmd->bhmqk', qb, k).astype(jnp.float32) * scale
        causal = t[:, None] >= pos[None, :]
        p = jax.nn.softmax(jnp.where(causal, s, NEG_INF), axis=-1)
        p = p[:, :, 0] - lam * p[:, :, 1]
        return jnp.einsum('bhqk,bkhe->bqhe', p, v)

    out = lax.map(block, (to_blocks(q), pos.reshape(S // Q_BLOCK, Q_BLOCK)))
    return from_blocks(out)


def dsa_attention(q, k, v, qi, ki, wi, pos, topk):
    scale = DSA_DIM ** -0.5
    S = q.shape[1]

    def block(args):
        qb, qib, wib, t = args
        sc = jnp.einsum('bqhd,bsd->bqhs', qib, ki).astype(jnp.float32)
        idx_score = jnp.einsum('bqhs,bqh->bqs', jax.nn.relu(sc), wib.astype(jnp.float32))
        causal = t[:, None] >= pos[None, :]
        idx_score = jnp.where(causal[None], idx_score, -jnp.inf)
        _, idx = lax.top_k(idx_score, topk)
        kg = jax.vmap(lambda kk, ii: kk[ii])(k, idx)
        vg = jax.vmap(lambda vv, ii: vv[ii])(v, idx)
        s = jnp.einsum('bqhd,bqkd->bhqk', qb, kg).astype(jnp.float32) * scale
        valid = idx <= t[None, :, None]
        p = jax.nn.softmax(jnp.where(valid[:, None], s, NEG_INF), axis=-1)
        return jnp.einsum('bhqk,bqkd->bqhd', p, vg)

    out = lax.map(block, (to_blocks(q), to_blocks(qi), to_blocks(wi),
                          pos.reshape(S // Q_BLOCK, Q_BLOCK)))
    return from_blocks(out)


def forgetting_attention(q, k, v, logf, pos):
    scale = FOX_DIM ** -0.5
    S = q.shape[1]
    c = jnp.cumsum(logf, axis=1)
    c_keys = jnp.transpose(c, (0, 2, 1))

    def block(args):
        qb, cb, t = args
        s = jnp.einsum('bqhd,bkhd->bhqk', qb, k).astype(jnp.float32) * scale
        s = s + jnp.transpose(cb, (0, 2, 1))[..., None] - c_keys[:, :, None, :]
        causal = t[:, None] >= pos[None, :]
        p = jax.nn.softmax(jnp.where(causal, s, NEG_INF), axis=-1)
        return jnp.einsum('bhqk,bkhd->bqhd', p, v)

    out = lax.map(block, (to_blocks(q), to_blocks(c), pos.reshape(S // Q_BLOCK, Q_BLOCK)))
    return from_blocks(out)


def hybrid_layer(x, w_in, b_f, lam_q1, lam_k1, lam_q2, lam_k2, g_subln, w_out, ln_g, ln_b, layer_idx):
    B, S, _ = x.shape
    pos = jnp.arange(S, dtype=jnp.int32)
    h = jnp.einsum('bsd,de->bse', x, w_in)
    (dq, dk, dv, sq, sk, sv, iq, ik, iw, fq, fk, fv, ff, gate) = split_columns(h)

    dq = rope(dq.reshape(B, S, DIFF_HEADS * 2, DIFF_QK_DIM), pos).reshape(B, S, DIFF_HEADS, 2, DIFF_QK_DIM)
    dk = rope(dk.reshape(B, S, DIFF_HEADS * 2, DIFF_QK_DIM), pos).reshape(B, S, DIFF_HEADS, 2, DIFF_QK_DIM)
    dv = dv.reshape(B, S, DIFF_HEADS, DIFF_V_DIM)
    lambda_init = 0.8 - 0.6 * math.exp(-0.3 * layer_idx)
    lam = (jnp.exp(jnp.sum(lam_q1.astype(jnp.float32) * lam_k1.astype(jnp.float32)))
           - jnp.exp(jnp.sum(lam_q2.astype(jnp.float32) * lam_k2.astype(jnp.float32))) + lambda_init)
    a = diff_attention(dq, dk, dv, lam, pos)
    a = a * lax.rsqrt(jnp.mean(a * a, axis=-1, keepdims=True) + SUBLN_EPS) * g_subln.astype(jnp.float32) * (1.0 - lambda_init)
    a = a.reshape(B, S, DIFF_WIDTH)

    sq = rope(sq.reshape(B, S, DSA_HEADS, DSA_DIM), pos)
    sk = rope(sk, pos)
    iq = rope(iq.reshape(B, S, IDX_HEADS, IDX_DIM), pos)
    ik = rope(ik, pos)
    topk = min(DSA_TOPK_MAX, S // 4)
    bo = dsa_attention(sq, sk, sv, iq, ik, iw * IDX_WEIGHT_SCALE, pos, topk).reshape(B, S, DSA_WIDTH)

    logf = jax.nn.log_sigmoid(ff.astype(jnp.float32) + b_f.astype(jnp.float32))
    co = forgetting_attention(fq.reshape(B, S, FOX_HEADS, FOX_DIM), fk.reshape(B, S, FOX_HEADS, FOX_DIM),
                              fv.reshape(B, S, FOX_HEADS, FOX_DIM), logf, pos).reshape(B, S, FOX_WIDTH)

    mixed = jnp.concatenate([a, bo, co], axis=-1) * jax.nn.silu(gate.astype(jnp.float32))
    out = jnp.einsum('bse,ed->bsd', mixed.astype(x.dtype), w_out)
    return layer_norm(DEEPNORM_ALPHA * x + out, ln_g, ln_b)


def setup_inputs(seed: int = 0) -> dict:
    key = jax.random.key(seed)
    ks = jax.random.split(key, 12)
    x = jax.random.normal(ks[0], (BATCH, SEQ, D_MODEL), jnp.float32)
    col_scale = np.concatenate([
        np.full((n,), DEEPNORM_BETA if name in VALUE_COLS else 1.0, np.float32) for name, n in IN_SPLITS
    ]) * np.float32(D_MODEL ** -0.5)
    w_in = jax.random.normal(ks[1], (DEPTH, D_MODEL, IN_WIDTH), jnp.float32) * jnp.asarray(col_scale)
    b_f = 1.0 + 0.1 * jax.random.normal(ks[2], (DEPTH, FOX_HEADS), jnp.float32)
    lam_q1 = 0.1 * jax.random.normal(ks[3], (DEPTH, DIFF_QK_DIM), jnp.float32)
    lam_k1 = 0.1 * jax.random.normal(ks[4], (DEPTH, DIFF_QK_DIM), jnp.float32)
    lam_q2 = 0.1 * jax.random.normal(ks[5], (DEPTH, DIFF_QK_DIM), jnp.float32)
    lam_k2 = 0.1 * jax.random.normal(ks[6], (DEPTH, DIFF_QK_DIM), jnp.float32)
    g_subln = 1.0 + 0.02 * jax.random.normal(ks[7], (DEPTH, DIFF_V_DIM), jnp.float32)
    w_out = jax.random.normal(ks[8], (DEPTH, D_MIX, D_MODEL), jnp.float32) * (D_MIX ** -0.5) * DEEPNORM_BETA
    ln_g = 1.0 + 0.02 * jax.random.normal(ks[9], (DEPTH, D_MODEL), jnp.float32)
    ln_b = 0.02 * jax.random.normal(ks[10], (DEPTH, D_MODEL), jnp.float32)
    return {'x': x, 'w_in': w_in, 'b_f': b_f, 'lam_q1': lam_q1, 'lam_k1': lam_k1,
            'lam_q2': lam_q2, 'lam_k2': lam_k2, 'g_subln': g_subln, 'w_out': w_out,
            'ln_g': ln_g, 'ln_b': ln_b}


def reference(x, w_in, b_f, lam_q1, lam_k1, lam_q2, lam_k2, g_subln, w_out, ln_g, ln_b):
    for l in range(DEPTH):
        x = hybrid_layer(x, w_in[l], b_f[l], lam_q1[l], lam_k1[l], lam_q2[l], lam_k2[l],
                         g_subln[l], w_out[l], ln_g[l], ln_b[l], l)
    return x
```

```python
import contextlib
import math
import numpy as np
import ml_dtypes
import concourse.bass as bass
import concourse.mybir as mybir
from concourse.bass_utils import run_bass_kernel_spmd

F32 = mybir.dt.float32
BF16 = mybir.dt.bfloat16
AF = mybir.ActivationFunctionType
ALU = mybir.AluOpType
AX = mybir.AxisListType
NPBF = ml_dtypes.bfloat16

D_MODEL = 1024
DEPTH = 4
IN_WIDTH = 4300
NEG = -30000.0
LN_EPS = 1e-5
SUBLN_EPS = 1e-5
ALPHA = (2 * DEPTH) ** 0.25
IDX_WEIGHT_SCALE = 512.0 ** -0.5
TOPK = 256
BIS_ITERS = 24
BIS_W0 = 32.0

ENGS = ("pe", "act", "dve", "pool", "sp")
SAME_ENGINE_RAW = True
SEM_CAP = 2000


class Trk:
    __slots__ = ("w", "r", "name")

    def __init__(self, name=""):
        self.w = []
        self.r = []
        self.name = name


class Rec:
    __slots__ = ("eng", "fn", "dma", "deps", "signal", "semkey", "val")

    def __init__(self, eng, fn, dma, semkey):
        self.eng = eng
        self.fn = fn
        self.dma = dma
        self.deps = []
        self.signal = False
        self.semkey = semkey
        self.val = None


def _need(d, rec, kind):
    if d.dma or rec.dma:
        return True
    if d.eng != rec.eng:
        return True
    if rec.eng == "pe":
        return False
    if kind == "raw":
        return SAME_ENGINE_RAW
    return False


class Prog:
    def __init__(self, nc):
        self.nc = nc
        self.q = {e: [] for e in ENGS}

    def op(self, eng, fn, reads=(), writes=(), dma_key=None, acc=False):
        dma = dma_key is not None
        rec = Rec(eng, fn, dma, ("dma", dma_key) if dma else eng)
        deps = {}
        for t in reads:
            for d in t.w:
                if _need(d, rec, "raw"):
                    deps[id(d)] = d
        for t in writes:
            if not acc:
                for d in t.w:
                    if _need(d, rec, "waw"):
                        deps[id(d)] = d
            for d in t.r:
                if _need(d, rec, "war"):
                    deps[id(d)] = d
        rec.deps = list(deps.values())
        for d in rec.deps:
            d.signal = True
        for t in reads:
            t.r.append(rec)
        for t in writes:
            if acc:
                t.w.append(rec)
            else:
                t.w = [rec]
            t.r = []
        self.q[eng].append(rec)
        return rec

    def emit(self, final_waits=()):
        nc = self.nc
        for r in final_waits:
            r.signal = True
        counts = {}
        for e in ENGS:
            for rec in self.q[e]:
                if rec.signal:
                    k = rec.semkey
                    ep, cnt = counts.get(k, (0, 0))
                    inc = 16 if rec.dma else 1
                    if cnt + inc > SEM_CAP:
                        ep += 1
                        cnt = 0
                    cnt += inc
                    counts[k] = (ep, cnt)
                    rec.semkey = (k, ep)
                    rec.val = cnt
        allkeys = []
        for k, (ep, cnt) in counts.items():
            for i in range(ep + 1):
                allkeys.append((k, i))
        with contextlib.ExitStack() as es:
            sems = {}
            for i, k in enumerate(allkeys):
                sems[k] = es.enter_context(nc.semaphore("s%d" % i))
            block = es.enter_context(nc.Block())

            def run(ename):
                def body(engobj):
                    waited = {}
                    for rec in self.q[ename]:
                        for d in rec.deps:
                            k = d.semkey
                            if waited.get(k, 0) >= d.val:
                                continue
                            engobj.wait_ge(sems[k], d.val)
                            waited[k] = d.val
                        ins = rec.fn(engobj)
                        if rec.signal:
                            ins.then_inc(sems[rec.semkey], 16 if rec.dma else 1)
                    if ename == "sp":
                        for r in final_waits:
                            if waited.get(r.semkey, 0) >= r.val:
                                continue
                            engobj.wait_ge(sems[r.semkey], r.val)
                            waited[r.semkey] = r.val
                return body

            block.tensor(run("pe"))
            block.scalar(run("act"))
            block.vector(run("dve"))
            block.gpsimd(run("pool"))
            block.sync(run("sp"))
        return {k: len(allkeys) for k in ["nsems"]}


class Buf:
    __slots__ = ("t", "k")

    def __init__(self, t, name):
        self.t = t
        self.k = Trk(name)


class Ring:
    def __init__(self, bufs):
        self.bufs = bufs
        self.i = 0

    def next(self):
        b = self.bufs[self.i % len(self.bufs)]
        self.i += 1
        return b


class Ctx:
    def __init__(self, nc):
        self.nc = nc
        self.es = contextlib.ExitStack()
        self.p = Prog(nc)
        self.outs = []

    def sb(self, name, shape, dt):
        return Buf(self.es.enter_context(self.nc.sbuf_tensor(name, shape, dt)), name)

    def ps(self, name, shape, dt=F32):
        return Buf(self.es.enter_context(self.nc.psum_tensor(name, shape, dt)), name)

    def load(self, buf, out, in_, acc=False, reads=()):
        return self.p.op("sp", lambda e: e.dma_start(out=out, in_=in_), reads=reads, writes=[buf.k],
                         dma_key=buf.k.name, acc=acc)

    def store(self, buf, out, in_, writes=()):
        r = self.p.op("sp", lambda e: e.dma_start(out=out, in_=in_), reads=[buf.k], writes=writes,
                      dma_key=buf.k.name)
        return r

    def mm(self, ob, out, lb, lhsT, rb, rhs, start, stop):
        return self.p.op("pe", lambda e: e.matmul(out, lhsT=lhsT, rhs=rhs, start=start, stop=stop),
                         reads=[lb.k, rb.k], writes=[ob.k])

    def op(self, eng, fn, reads, writes):
        return self.p.op(eng, fn, reads=[b.k for b in reads], writes=[b.k for b in writes])

    def tt(self, eng, ob, out, ab, a, bb, b, op):
        return self.op(eng, lambda e: e.tensor_tensor(out=out, in0=a, in1=b, op=op), [ab, bb], [ob])

    def ts(self, eng, ob, out, ib, in0, s1, s2, op0, op1=None, extra=(), accum=None):
        kw = {}
        if op1 is not None:
            kw["op1"] = op1
        if accum is not None:
            kw["accum_out"] = accum
        return self.op(eng, lambda e: e.tensor_scalar(out=out, in0=in0, scalar1=s1, scalar2=s2, op0=op0, **kw),
                       [ib] + list(extra), [ob])

    def stt(self, ob, out, ib, in0, scalar, jb, in1, op0, op1, extra=()):
        return self.op("dve", lambda e: e.scalar_tensor_tensor(out=out, in0=in0, scalar=scalar, in1=in1, op0=op0, op1=op1),
                       [ib, jb] + list(extra), [ob])

    def actf(self, ob, out, ib, in_, func, scale=1.0, bias=0.0, extra=()):
        return self.op("act", lambda e: e.activation(out=out, in_=in_, func=func, scale=scale, bias=bias),
                       [ib] + list(extra), [ob])

    def cp(self, eng, ob, out, ib, in_):
        if eng == "act":
            return self.op("act", lambda e: e.copy(out=out, in_=in_), [ib], [ob])
        return self.op(eng, lambda e: e.tensor_copy(out=out, in_=in_), [ib], [ob])


def dram(nc, name, shape, dt, kind):
    return nc.dram_tensor(name, list(shape), dt, kind=kind).ap()


def sw_col(c):
    if c < 1024:
        return c
    if 1536 <= c < 1856:
        return 1024 + (c - 1536)
    assert 1920 <= c < 2496
    return 1344 + (c - 1920)


def build_A(NT):
    T = NT * 512
    nc = bass.Bass("TRN2", target_bir_lowering=False)
    I, O = "ExternalInput", "ExternalOutput"
    xT = dram(nc, "xT", [1024, T], F32, I)
    w = dram(nc, "w", [1024, IN_WIDTH], F32, I)
    wsw = dram(nc, "wsw", [1024, 1920], F32, I)
    bfv = dram(nc, "bfv", [4, 1], F32, I)
    cosT = dram(nc, "cosT", [128, T], F32, I)
    sinT = dram(nc, "sinT", [128, T], F32, I)
    qd = dram(nc, "qd", [128, 4, T], BF16, O)
    kd = dram(nc, "kd", [128, 4, T], BF16, O)
    vd = dram(nc, "vd", [T, 512], BF16, O)
    qs = dram(nc, "qs", [64, T // 128, 4, 128], BF16, O)
    ks = dram(nc, "ks", [64, T], BF16, O)
    vs = dram(nc, "vs", [T, 64], BF16, O)
    qi = dram(nc, "qi", [64, 8, T], BF16, O)
    ki = dram(nc, "ki", [64, T], BF16, O)
    wi = dram(nc, "wi", [T, 8], F32, O)
    qf = dram(nc, "qf", [64, 4, T], BF16, O)
    kf = dram(nc, "kf", [64, 4, T], BF16, O)
    vf = dram(nc, "vf", [T, 256], BF16, O)
    cl = dram(nc, "cl", [4, T], F32, O)
    caug = dram(nc, "caug", [4, 3, T], BF16, O)
    gd = dram(nc, "gd", [128, 4, T], BF16, O)
    gs = dram(nc, "gs", [64, 8, T], BF16, O)

    c = Ctx(nc)
    with c.es:
        Wb = c.sb("Wb", [128, 8, IN_WIDTH], BF16)
        Wsb = c.sb("Wsb", [128, 8, 1920], BF16)
        wst = Ring([c.sb("wst%d" % i, [128, 2150], F32) for i in range(2)])
        cosr = Ring([c.sb("cosb%d" % i, [128, 512], F32) for i in range(2)])
        sinr = Ring([c.sb("sinb%d" % i, [128, 512], F32) for i in range(2)])
        bfb = c.sb("bfb", [4, 1], F32)
        nbfb = c.sb("nbfb", [4, 1], F32)
        ones4 = c.sb("ones4", [4, 512], F32)
        xst = Ring([c.sb("xst%d" % i, [128, 8, 512], F32) for i in range(1)])
        xbr = Ring([c.sb("xb%d" % i, [128, 8, 512], BF16) for i in range(2)])
        PS = Ring([c.ps("ps%d" % i, [128, 512]) for i in range(8)])
        st16 = Ring([c.sb("st16_%d" % i, [128, 512], BF16) for i in range(6)])
        t1r = Ring([c.sb("t1_%d" % i, [128, 512], F32) for i in range(3)])
        t2r = Ring([c.sb("t2_%d" % i, [128, 512], F32) for i in range(3)])
        wir = Ring([c.sb("wis%d" % i, [128, 8], F32) for i in range(2)])
        f4 = [c.sb("f4_%d" % i, [4, 512], F32) for i in range(7)]
        h3 = [c.sb("h3_%d" % i, [4, 512], BF16) for i in range(3)]
        outs = []

        c.load(bfb, bfb.t[:], bfv[:, :])
        c.op("dve", lambda e: e.tensor_scalar(out=nbfb.t[:], in0=bfb.t[:], scalar1=-1.0, scalar2=None, op0=ALU.mult),
             [bfb], [nbfb])
        c.op("pool", lambda e: e.memset(ones4.t[:], 1.0), [], [ones4])
        cast_engs = ["dve", "pool", "act"]
        ci = 0
        for ch in range(8):
            for half in range(2):
                s = wst.next()
                h0 = half * 2150
                c.load(s, s.t[:], w[ch * 128:(ch + 1) * 128, h0:h0 + 2150])
                for (a, b) in [(0, 1075), (1075, 2150)]:
                    eng = cast_engs[ci % 3]
                    ci += 1
                    if eng == "act":
                        c.op("act", lambda e, s=s, ch=ch, a=a, b=b, h0=h0: e.copy(out=Wb.t[:, ch, h0 + a:h0 + b], in_=s.t[:, a:b]), [s], [Wb])
                    else:
                        c.op(eng, lambda e, s=s, ch=ch, a=a, b=b, h0=h0: e.tensor_copy(out=Wb.t[:, ch, h0 + a:h0 + b], in_=s.t[:, a:b]), [s], [Wb])
            s = wst.next()
            c.load(s, s.t[:, 0:1920], wsw[ch * 128:(ch + 1) * 128, :])
            for (a, b) in [(0, 960), (960, 1920)]:
                eng = cast_engs[ci % 2]
                ci += 1
                c.op(eng, lambda e, s=s, ch=ch, a=a, b=b: e.tensor_copy(out=Wsb.t[:, ch, a:b], in_=s.t[:, a:b]), [s], [Wsb])

        def fm(xb, col0, M, sw=False):
            pb = PS.next()
            src = Wsb if sw else Wb
            c0 = sw_col(col0) if sw else col0
            for ch in range(8):
                c.mm(pb, pb.t[0:M, :], src, src.t[:, ch, c0:c0 + M], xb, xb.t[:, ch, :], ch == 0, ch == 7)
            return pb

        for j in range(NT):
            tok = slice(j * 512, (j + 1) * 512)
            xs = xst.next()
            c.load(xs, xs.t[:], xT[:, tok].rearrange("(c p) t -> p c t", p=128))
            cosb = cosr.next()
            sinb = sinr.next()
            c.load(cosb, cosb.t[:], cosT[:, tok])
            c.load(sinb, sinb.t[:], sinT[:, tok])
            xb = xbr.next()
            c.op("dve", lambda e, xs=xs, xb=xb: e.tensor_copy(out=xb.t[:, 0:4, :], in_=xs.t[:, 0:4, :]), [xs], [xb])
            c.op("pool", lambda e, xs=xs, xb=xb: e.tensor_copy(out=xb.t[:, 4:8, :], in_=xs.t[:, 4:8, :]), [xs], [xb])

            def rope_group(col0, M, dst, xb=xb, cosb=cosb, sinb=sinb):
                ph = fm(xb, col0, M)
                phs = fm(xb, col0, M, sw=True)
                t1 = t1r.next()
                t2 = t2r.next()
                so = st16.next()
                c.op("dve", lambda e: e.tensor_tensor(out=t1.t[0:M, :], in0=ph.t[0:M, :], in1=cosb.t[0:M, :], op=ALU.mult),
                     [ph, cosb], [t1])
                c.op("dve", lambda e: e.tensor_tensor(out=t2.t[0:M, :], in0=phs.t[0:M, :], in1=sinb.t[0:M, :], op=ALU.mult),
                     [phs, sinb], [t2])
                c.op("pool", lambda e: e.tensor_tensor(out=so.t[0:M, :], in0=t1.t[0:M, :], in1=t2.t[0:M, :], op=ALU.add),
                     [t1, t2], [so])
                outs.append(c.store(so, dst, so.t[0:M, :] if len(dst.shape) == 2 else so.t[0:M, :].rearrange("p (b i) -> p b i", i=128)))

            def plain_group(col0, M, dst, silu, xb=xb):
                ph = fm(xb, col0, M)
                so = st16.next()
                if silu:
                    c.op("act", lambda e: e.activation(out=so.t[0:M, :], in_=ph.t[0:M, :], func=AF.Silu), [ph], [so])
                else:
                    c.op("act", lambda e: e.copy(out=so.t[0:M, :], in_=ph.t[0:M, :]), [ph], [so])
                outs.append(c.store(so, dst, so.t[0:M, :]))

            for h in range(4):
                rope_group(h * 128, 128, qd[:, h, tok])
            for h in range(4):
                rope_group(512 + h * 128, 128, kd[:, h, tok])
            for h in range(4):
                rope_group(1536 + h * 64, 64, qs[:, j * 4:(j + 1) * 4, h, :])
            rope_group(1792, 64, ks[:, tok])
            for h in range(8):
                rope_group(1920 + h * 64, 64, qi[:, h, tok])
            rope_group(2432, 64, ki[:, tok])
            for h in range(4):
                plain_group(2504 + h * 64, 64, qf[:, h, tok], False)
            for h in range(4):
                plain_group(2760 + h * 64, 64, kf[:, h, tok], False)
            ph = fm(xb, 3272, 4)
            c.op("act", lambda e, ph=ph: e.activation(out=f4[0].t[:], in_=ph.t[0:4, :], func=AF.Exp, scale=-1.0, bias=nbfb.t[:]),
                 [ph, nbfb], [f4[0]])
            c.op("act", lambda e: e.activation(out=f4[1].t[:], in_=f4[0].t[:], func=AF.Ln, bias=1.0, scale=1.0), [f4[0]], [f4[1]])
            c.op("dve", lambda e: e.tensor_scalar(out=f4[2].t[:], in0=f4[1].t[:], scalar1=-1.0, scalar2=None, op0=ALU.mult),
                 [f4[1]], [f4[2]])
            c.op("dve", lambda e: e.tensor_tensor_scan(out=f4[3].t[:], data0=ones4.t[:], data1=f4[2].t[:], initial=0.0,
                                                        op0=ALU.mult, op1=ALU.add), [f4[2], ones4], [f4[3]])
            outs.append(c.store(f4[3], cl[:, tok], f4[3].t[:]))
            c.ts("dve", f4[4], f4[4].t[:], f4[3], f4[3].t[:], 8.0, None, ALU.mult)
            c.cp("dve", h3[0], h3[0].t[:], f4[4], f4[4].t[:])
            c.tt("dve", f4[5], f4[5].t[:], f4[4], f4[4].t[:], h3[0], h3[0].t[:], ALU.subtract)
            c.cp("dve", h3[1], h3[1].t[:], f4[5], f4[5].t[:])
            c.tt("dve", f4[6], f4[6].t[:], f4[5], f4[5].t[:], h3[1], h3[1].t[:], ALU.subtract)
            c.cp("dve", h3[2], h3[2].t[:], f4[6], f4[6].t[:])
            for i3 in range(3):
                outs.append(c.store(h3[i3], caug[:, i3, tok], h3[i3].t[:]))
            for h in range(4):
                plain_group(3276 + h * 128, 128, gd[:, h, tok], True)
            for i in range(8):
                plain_group(3276 + 512 + i * 64, 64, gs[:, i, tok], True)
            for b in range(4):
                rows = slice(j * 512 + b * 128, j * 512 + (b + 1) * 128)
                xcols = slice(b * 128, (b + 1) * 128)

                def tm(col0, N):
                    pb = PS.next()
                    for ch in range(8):
                        c.mm(pb, pb.t[:, 0:N], xb, xb.t[:, ch, xcols], Wb, Wb.t[:, ch, col0:col0 + N], ch == 0, ch == 7)
                    return pb

                pb = tm(1024, 512)
                so = st16.next()
                c.op("act", lambda e, pb=pb, so=so: e.copy(out=so.t[:, :], in_=pb.t[:, :]), [pb], [so])
                outs.append(c.store(so, vd[rows, :], so.t[:, :]))
                pb = tm(1856, 64)
                so = st16.next()
                c.op("act", lambda e, pb=pb, so=so: e.copy(out=so.t[:, 0:64], in_=pb.t[:, 0:64]), [pb], [so])
                outs.append(c.store(so, vs[rows, :], so.t[:, 0:64]))
                pb = tm(3016, 256)
                so = st16.next()
                c.op("act", lambda e, pb=pb, so=so: e.copy(out=so.t[:, 0:256], in_=pb.t[:, 0:256]), [pb], [so])
                outs.append(c.store(so, vf[rows, :], so.t[:, 0:256]))
                pb = tm(2496, 8)
                ws = wir.next()
                c.op("dve", lambda e, pb=pb, ws=ws: e.tensor_scalar(out=ws.t[:, :], in0=pb.t[:, 0:8], scalar1=IDX_WEIGHT_SCALE,
                                                                    scalar2=None, op0=ALU.mult), [pb], [ws])
                outs.append(c.store(ws, wi[rows, :], ws.t[:, :]))
        c.p.emit(final_waits=outs)
    return nc


def core_positions(NT, r):
    return np.concatenate([(4 * j + r) * 512 + np.arange(512) for j in range(NT)])


def rope_tables(NT, r):
    pos = core_positions(NT, r).astype(np.float32)
    inv = np.float32(10000.0) ** (-np.arange(0, 64, 2, dtype=np.float32) / np.float32(64))
    ang = (pos[None, :] * inv[:, None]).astype(np.float32)
    cos = np.cos(ang).astype(np.float32)
    sin = np.sin(ang).astype(np.float32)
    sgn = np.where((np.arange(128) % 64) < 32, -1.0, 1.0).astype(np.float32)
    return np.ascontiguousarray(np.tile(cos, (4, 1))), np.ascontiguousarray(np.tile(sin, (4, 1)) * sgn[:, None])


def sw_perm():
    cols = np.concatenate([np.arange(0, 1024), np.arange(1536, 1856), np.arange(1920, 2496)])
    return (cols // 64) * 64 + ((cols % 64) + 32) % 64


def shard_xT(x, NT):
    out = []
    for b in range(2):
        xb = x[b].reshape(NT, 4, 512, 1024)
        for r in range(4):
            out.append(np.ascontiguousarray(xb[:, r].reshape(NT * 512, 1024).T))
    return out


_PROGS = {}


def get_prog(name, NT):
    key = (name, NT)
    if key not in _PROGS:
        _PROGS[key] = build_A(NT) if name == "A" else build_B(NT)
    return _PROGS[key]


def run_A(xTs, w_in_l, b_f_l, NT):
    nc = get_prog("A", NT)
    wsw = np.ascontiguousarray(w_in_l[:, sw_perm()])
    maps = []
    for cidx in range(8):
        r = cidx % 4
        cosT, sinT = rope_tables(NT, r)
        maps.append({"xT": xTs[cidx], "w": np.ascontiguousarray(w_in_l), "wsw": wsw,
                     "bfv": np.ascontiguousarray(b_f_l.reshape(4, 1)), "cosT": cosT, "sinT": sinT})
    res = run_bass_kernel_spmd(nc, maps, core_ids=list(range(8)))
    return res.results


def build_B(NT):
    T = NT * 512
    NKT = 4 * NT
    S = NKT * 512
    NCK = NT
    SCW = max(S, 8192 + 6144)
    nc = bass.Bass("TRN2", target_bir_lowering=False)
    I, O = "ExternalInput", "ExternalOutput"
    qd = dram(nc, "qd", [128, 4, T], BF16, I)
    qf = dram(nc, "qf", [64, 4, T], BF16, I)
    caug_o = dram(nc, "caug_o", [4, 3, T], BF16, I)
    qs = dram(nc, "qs", [64, T // 128, 4, 128], BF16, I)
    qi = dram(nc, "qi", [64, 8, T], BF16, I)
    wi = dram(nc, "wi", [T, 8], F32, I)
    gd = dram(nc, "gd", [128, 4, T], BF16, I)
    gs = dram(nc, "gs", [64, 8, T], BF16, I)
    xT = dram(nc, "xT", [1024, T], F32, I)
    kd_f = dram(nc, "kd_f", [128, 4, S], BF16, I)
    vd_f = dram(nc, "vd_f", [4, NCK, 128, 16, 128], BF16, I)
    kf_f = dram(nc, "kf_f", [64, 4, S], BF16, I)
    caug_f = dram(nc, "caug_f", [4, 3, S], BF16, I)
    vf_f = dram(nc, "vf_f", [4, NCK, 128, 16, 64], BF16, I)
    ks_f = dram(nc, "ks_f", [64, S], BF16, I)
    vs_f = dram(nc, "vs_f", [NCK, 128, 16, 64], BF16, I)
    ki_f = dram(nc, "ki_f", [64, S], BF16, I)
    tot_f = dram(nc, "tot_f", [128, 4, NKT], F32, I)
    mr_d = dram(nc, "mr", [128, 4], F32, I)
    wout = dram(nc, "wout", [1024, 1024], F32, I)
    lng_d = dram(nc, "lng", [128, 8], F32, I)
    lnb_d = dram(nc, "lnb", [128, 8], F32, I)
    gsub_d = dram(nc, "gsub", [128, 1], F32, I)
    lamv_d = dram(nc, "lamv", [128, 256], F32, I)
    lcon_d = dram(nc, "lcon", [128, 2], F32, I)
    maskT_d = dram(nc, "maskT", [128, 16, 512], BF16, I)
    maskA_d = dram(nc, "maskA", [128, 16, 512], BF16, I)
    ident_d = dram(nc, "ident", [128, 128], BF16, I)
    xo = dram(nc, "xo", [1024, T], F32, O)

    c = Ctx(nc)
    with c.es:
        score = c.sb("score", [128, SCW], F32)
        junk = c.sb("junk", [128, 1024], BF16)
        ident = c.sb("identb", [128, 128], BF16)
        onesb = c.sb("onesb", [128, 128], BF16)
        onesf = c.sb("onesf", [128, 128], F32)
        mask = c.sb("maskb", [128, 16, 512], BF16)
        lng = c.sb("lngb", [128, 8], F32)
        lnb = c.sb("lnbb", [128, 8], F32)
        gsub = c.sb("gsubb", [128, 1], F32)
        lamv = c.sb("lamvb", [128, 256], F32)
        lcon = c.sb("lconb", [128, 2], F32)
        mr = c.sb("mrb", [128, 4], F32)
        tot = c.sb("totb", [128, 4, NKT], F32)
        base = c.sb("baseb", [128, 4, NKT], F32)
        ones_k = c.sb("ones_k", [128, NKT], F32)
        bown = c.sb("bown", [128, 4, NT], F32)
        tab = c.sb("tab", [128, 4, NT, NKT], F32)
        sm = c.sb("sm", [128, 16], F32)
        lt = c.sb("lt", [128, 64], F32)
        QD = c.sb("QD", [128, 4, 512], BF16)
        QF = c.sb("QF", [70, 4, 512], BF16)
        QS = c.sb("QS", [64, 4, 4, 128], BF16)
        QI = c.sb("QI", [64, 8, 512], BF16)
        WI = c.sb("WI", [128, 4, 8], F32)
        GD = c.sb("GD", [128, 4, 512], BF16)
        GS = c.sb("GS", [64, 8, 512], BF16)
        KcD = Ring([c.sb("KcD%d" % i, [128, 2048], BF16) for i in range(2)])
        KcF = Ring([c.sb("KcF%d" % i, [70, 2048], BF16) for i in range(1)])
        Vc = Ring([c.sb("Vc%d" % i, [128, 16, 128], BF16) for i in range(2)])
        KIc = Ring([c.sb("KIc%d" % i, [64, 2048], BF16) for i in range(2)])
        KSc = KIc
        VSc = Ring([c.sb("VSc%d" % i, [128, 16, 64], BF16) for i in range(1)])
        Pr = Ring([c.sb("P%d" % i, [128, 512], BF16) for i in range(3)])
        mbr = Ring([c.sb("mb%d" % i, [128, 512], BF16) for i in range(2)])
        mbTr = Ring([c.sb("mbT%d" % i, [128, 4, 128], BF16) for i in range(2)])
        tmpr = Ring([c.sb("tmp%d" % i, [128, 512], F32) for i in range(2)])
        mixT = c.sb("mixT", [128, 12, 512], BF16)
        rl = c.sb("rl", [128, 512], F32)
        n0 = c.sb("n0", [128, 512], F32)
        n1 = c.sb("n1", [128, 512], F32)
        av = c.sb("av", [128, 512], F32)
        sq = c.sb("sq", [128, 512], F32)
        bs = c.sb("bs", [128, 8], F32)
        cntp = c.sb("cntp", [128, 16], F32)
        Sr = Ring([c.ps("S%d" % i, [128, 512]) for i in range(3)])
        Ob = c.ps("Ob", [128, 512])
        Lb = c.ps("Lb", [128, 512])
        Mb = c.ps("Mb", [128, 512])
        PSt = c.ps("PSt", [128, 512], BF16)
        outs = []

        c.load(ident, ident.t[:], ident_d[:, :])
        c.op("pool", lambda e: e.memset(onesb.t[:], 1.0), [], [onesb])
        c.op("pool", lambda e: e.memset(onesf.t[:], 1.0), [], [onesf])
        c.op("pool", lambda e: e.memset(ones_k.t[:], 1.0), [], [ones_k])
        c.op("pool", lambda e: e.memset(QF.t[64:70, :, :], -1.0), [], [QF])
        for kb_ in KcF.bufs:
            c.op("pool", lambda e, kb_=kb_: e.memset(kb_.t[64:70, :], 1.0), [], [kb_])
        for (b_, d_) in [(lng, lng_d), (lnb, lnb_d), (gsub, gsub_d), (lamv, lamv_d), (lcon, lcon_d), (mr, mr_d)]:
            c.load(b_, b_.t[:], d_[:, :])
        c.load(tot, tot.t[:], tot_f[:, :, :])
        c.tt("dve", lt, lt.t[:], lamv, lamv.t[:, 0:64], lamv, lamv.t[:, 64:128], ALU.mult)
        c.op("dve", lambda e: e.tensor_reduce(out=sm.t[:, 0:1], in_=lt.t[:], axis=AX.X, op=ALU.add), [lt], [sm])
        c.tt("dve", lt, lt.t[:], lamv, lamv.t[:, 128:192], lamv, lamv.t[:, 192:256], ALU.mult)
        c.op("dve", lambda e: e.tensor_reduce(out=sm.t[:, 1:2], in_=lt.t[:], axis=AX.X, op=ALU.add), [lt], [sm])
        c.actf(sm, sm.t[:, 2:4], sm, sm.t[:, 0:2], AF.Exp)
        c.tt("dve", sm, sm.t[:, 4:5], sm, sm.t[:, 2:3], sm, sm.t[:, 3:4], ALU.subtract)
        c.tt("dve", sm, sm.t[:, 4:5], sm, sm.t[:, 4:5], lcon, lcon.t[:, 0:1], ALU.add)
        c.ts("dve", sm, sm.t[:, 5:6], sm, sm.t[:, 4:5], -1.0, None, ALU.mult)
        c.tt("dve", sm, sm.t[:, 6:7], gsub, gsub.t[:, 0:1], lcon, lcon.t[:, 1:2], ALU.mult)
        for h in range(4):
            c.op("dve", lambda e, h=h: e.tensor_tensor_scan(out=base.t[:, h, :], data0=ones_k.t[:], data1=tot.t[:, h, :],
                                                             initial=0.0, op0=ALU.mult, op1=ALU.add), [tot, ones_k], [base])
        c.tt("dve", base, base.t[:], base, base.t[:], tot, tot.t[:], ALU.subtract)
        totv = tot.t[:].rearrange("p h (j r) -> p h j r", r=4)
        basev = base.t[:].rearrange("p h (j r) -> p h j r", r=4)
        for h in range(4):
            c.cp("dve", bown, bown.t[:, h, :], base, basev[:, h, :, 0])
            for rr in range(4):
                c.stt(bown, bown.t[:, h, :], tot, totv[:, h, :, rr], mr.t[:, rr:rr + 1], bown, bown.t[:, h, :], ALU.mult, ALU.add,
                      extra=[mr])
        for h in range(4):
            for J in range(NT):
                c.ts("dve", tab, tab.t[:, h, J, :], base, base.t[:, h, :], bown.t[:, h, J:J + 1], -1.0, ALU.subtract, ALU.mult,
                     extra=[bown])

        for J in range(NT):
            tok = slice(J * 512, (J + 1) * 512)
            nck = J + 1
            c.load(QD, QD.t[:], qd[:, :, tok])
            c.load(QF, QF.t[0:64, :, :], qf[:, :, tok])
            c.load(QF, QF.t[67:70, :, :], caug_o[:, :, tok].rearrange("h i t -> i h t"), acc=True)
            c.load(QS, QS.t[:], qs[:, J * 4:(J + 1) * 4, :, :])
            c.load(QI, QI.t[:], qi[:, :, tok])
            c.load(WI, WI.t[:], wi[tok, :].rearrange("(b p) h -> p b h", p=128))
            c.load(GD, GD.t[:], gd[:, :, tok])
            c.load(GS, GS.t[:], gs[:, :, tok])

            c.load(mask, mask.t[:], maskT_d[:, :, :])
            for hm in range(12):
                is_fox = hm >= 8
                if is_fox:
                    h = hm - 8
                    dv = 64
                    qop_b, qop = QF, QF.t[0:70, h, :]
                else:
                    h, mp = hm // 2, hm % 2
                    dv = 128
                    qop_b, qop = QD, QD.t[mp * 64:(mp + 1) * 64, h, :]
                steps = []
                loaded = {}

                def load_chunk(ck, is_fox=is_fox, h=h, mp=(0 if is_fox else mp)):
                    if ck in loaded:
                        return loaded[ck]
                    cols = slice(ck * 2048, (ck + 1) * 2048)
                    if is_fox:
                        kc = KcF.next()
                        c.load(kc, kc.t[0:64, :], kf_f[:, h, cols])
                        c.load(kc, kc.t[64:67, :], caug_f[h, :, cols], acc=True)
                        vc = Vc.next()
                        c.load(vc, vc.t[:, :, 0:64], vf_f[h, ck])
                        krows = slice(0, 70)
                    else:
                        kc = KcD.next()
                        krows = slice(mp * 64, (mp + 1) * 64)
                        c.load(kc, kc.t[krows, :], kd_f[krows, h, cols])
                        vc = Vc.next()
                        c.load(vc, vc.t[:, :, :], vd_f[h, ck])
                    loaded[ck] = (kc, vc, krows)
                    return loaded[ck]

                for ck in range(nck):
                    for kb in range(16):
                        steps.append((ck, kb))
                n = len(steps)
                pend = []
                LAG = 2
                for i in range(n + LAG):
                    if i < n:
                        ck, kb = steps[i]
                        kc, vc, krows = load_chunk(ck)
                        zone = (ck == J)
                        sb_ = Sr.next()
                        c.mm(sb_, sb_.t[:, :], kc, kc.t[krows, kb * 128:(kb + 1) * 128], qop_b, qop, True, not zone)
                        if zone:
                            c.mm(sb_, sb_.t[:, :], ident, ident.t[:, :], mask, mask.t[:, kb, :], False, True)
                        pb = Pr.next()
                        if is_fox:
                            kt = ck * 4 + kb // 4
                            c.actf(pb, pb.t[:, :], sb_, sb_.t[:, :], AF.Exp, scale=0.125, bias=tab.t[:, h, J, kt:kt + 1], extra=[tab])
                        else:
                            c.actf(pb, pb.t[:, :], sb_, sb_.t[:, :], AF.Exp, scale=0.125)
                        pend.append((pb, vc, kb))
                    if i >= LAG:
                        pb, vc, kb = pend[i - LAG]
                        first, last = (i - LAG == 0), (i - LAG == n - 1)
                        c.mm(Ob, Ob.t[0:dv, :], vc, vc.t[:, kb, 0:dv], pb, pb.t[:, :], first, last)
                        c.mm(Lb, Lb.t[0:dv, :], onesb, onesb.t[:, 0:dv], pb, pb.t[:, :], first, last)
                c.op("dve", lambda e, dv=dv: e.reciprocal(out=rl.t[0:dv, :], in_=Lb.t[0:dv, :]), [Lb], [rl])
                if is_fox:
                    tm_ = tmpr.next()
                    c.tt("dve", tm_, tm_.t[0:64, :], Ob, Ob.t[0:64, :], rl, rl.t[0:64, :], ALU.mult)
                    c.tt("dve", mixT, mixT.t[0:64, 8 + h, :], tm_, tm_.t[0:64, :], GS, GS.t[0:64, 4 + h, :], ALU.mult)
                elif mp == 0:
                    c.tt("dve", n0, n0.t[:, :], Ob, Ob.t[:, :], rl, rl.t[:, :], ALU.mult)
                else:
                    c.tt("dve", n1, n1.t[:, :], Ob, Ob.t[:, :], rl, rl.t[:, :], ALU.mult)
                    c.stt(av, av.t[:, :], n1, n1.t[:, :], sm.t[:, 5:6], n0, n0.t[:, :], ALU.mult, ALU.add, extra=[sm])
                    c.tt("pool", sq, sq.t[:, :], av, av.t[:, :], av, av.t[:, :], ALU.mult)
                    c.mm(Mb, Mb.t[:, :], onesf, onesf.t[:, :], sq, sq.t[:, :], True, True)
                    c.ts("dve", sq, sq.t[:, :], Mb, Mb.t[:, :], 1.0 / 128.0, SUBLN_EPS, ALU.mult, ALU.add)
                    c.actf(sq, sq.t[:, :], sq, sq.t[:, :], AF.Ln)
                    c.actf(sq, sq.t[:, :], sq, sq.t[:, :], AF.Exp, scale=-0.5)
                    c.tt("dve", av, av.t[:, :], av, av.t[:, :], sq, sq.t[:, :], ALU.mult)
                    c.stt(mixT, mixT.t[:, h, :], av, av.t[:, :], sm.t[:, 6:7], GD, GD.t[:, h, :], ALU.mult, ALU.mult, extra=[sm])

            c.load(mask, mask.t[:], maskA_d[:, :, :])
            nk = nck * 2048
            for qb in range(4):
                qcols = slice(qb * 128, (qb + 1) * 128)
                for ck in range(nck):
                    kic = KIc.next()
                    c.load(kic, kic.t[:, :], ki_f[:, ck * 2048:(ck + 1) * 2048])
                    for ktl in range(4):
                        sc_ap = score.t[:, ck * 2048 + ktl * 512: ck * 2048 + (ktl + 1) * 512]
                        for h in range(8):
                            sb_ = Sr.next()
                            c.mm(sb_, sb_.t[:, :], QI, QI.t[:, h, qcols], kic, kic.t[:, ktl * 512:(ktl + 1) * 512], True, True)
                            if h == 0:
                                c.ts("dve", score, sc_ap, sb_, sb_.t[:, :], 0.0, WI.t[:, qb, h:h + 1], ALU.max, ALU.mult, extra=[WI])
                            else:
                                tm_ = tmpr.next()
                                c.ts("dve", tm_, tm_.t[:, :], sb_, sb_.t[:, :], 0.0, WI.t[:, qb, h:h + 1], ALU.max, ALU.mult, extra=[WI])
                                c.tt("pool", score, sc_ap, score, sc_ap, tm_, tm_.t[:, :], ALU.add)
                        if ck == J:
                            c.tt("pool", score, sc_ap, score, sc_ap, mask, mask.t[:, qb * 4 + ktl, :], ALU.add)
                c.op("dve", lambda e, nk=nk: e.tensor_reduce(out=bs.t[:, 0:1], in_=score.t[:, 0:nk], axis=AX.X, op=ALU.max), [score], [bs])
                c.ts("dve", bs, bs.t[:, 1:2], bs, bs.t[:, 0:1], -2.0 * BIS_W0, None, ALU.add)
                npc = nk // 1024
                for it in range(BIS_ITERS):
                    wstep = BIS_W0 / (2.0 ** it)
                    c.ts("dve", bs, bs.t[:, 2:3], bs, bs.t[:, 1:2], wstep, None, ALU.add)
                    for pc in range(npc):
                        c.ts("dve", junk, junk.t[:, :], score, score.t[:, pc * 1024:(pc + 1) * 1024], bs.t[:, 2:3], 0.0,
                             ALU.is_ge, ALU.add, extra=[bs], accum=cntp.t[:, pc:pc + 1])
                        c.p.q["dve"][-1]
                    if npc > 1:
                        c.op("dve", lambda e, npc=npc: e.tensor_reduce(out=bs.t[:, 3:4], in_=cntp.t[:, 0:npc], axis=AX.X, op=ALU.add),
                             [junk, cntp], [bs])
                        c.ts("dve", bs, bs.t[:, 4:5], bs, bs.t[:, 3:4], TOPK - 0.5, wstep, ALU.is_ge, ALU.mult)
                    else:
                        c.ts("dve", bs, bs.t[:, 4:5], junk, cntp.t[:, 0:1], TOPK - 0.5, wstep, ALU.is_ge, ALU.mult)
                    c.tt("dve", bs, bs.t[:, 1:2], bs, bs.t[:, 1:2], bs, bs.t[:, 4:5], ALU.add)
                steps = []
                for ck in range(nck):
                    for ktl in range(4):
                        steps.append((ck, ktl))
                nst = len(steps) * 4
                cnt_ = 0
                qs_ap = QS.t[:, qb, :, :].rearrange("p h i -> p (h i)")
                ksc = vsc = None
                for (ck, ktl) in steps:
                    if ktl == 0:
                        ksc = KSc.next()
                        c.load(ksc, ksc.t[:, :], ks_f[:, ck * 2048:(ck + 1) * 2048])
                        vsc = VSc.next()
                        c.load(vsc, vsc.t[:, :, :], vs_f[ck])
                    sc_ap = score.t[:, ck * 2048 + ktl * 512: ck * 2048 + (ktl + 1) * 512]
                    mb = mbr.next()
                    c.ts("dve", mb, mb.t[:, :], score, sc_ap, bs.t[:, 1:2], NEG, ALU.is_lt, ALU.mult, extra=[bs])
                    for kbl in range(4):
                        c.op("pe", lambda e, kbl=kbl, mb=mb: e.transpose(out=PSt.t[:, kbl * 128:(kbl + 1) * 128],
                                                                          in_=mb.t[:, kbl * 128:(kbl + 1) * 128], identity=ident.t[:, :]),
                             [mb, ident], [PSt])
                    mbT = mbTr.next()
                    c.cp("act", mbT, mbT.t[:, :, :].rearrange("p a b -> p (a b)"), PSt, PSt.t[:, :])
                    for kbl in range(4):
                        kb = ktl * 4 + kbl
                        sb_ = Sr.next()
                        c.mm(sb_, sb_.t[:, :], ksc, ksc.t[:, kb * 128:(kb + 1) * 128], QS, qs_ap, True, False)
                        for h in range(4):
                            c.mm(sb_, sb_.t[:, h * 128:(h + 1) * 128], ident, ident.t[:, :], mbT, mbT.t[:, kbl, :], False, True)
                        pb = Pr.next()
                        c.actf(pb, pb.t[:, :], sb_, sb_.t[:, :], AF.Exp, scale=0.125)
                        first, last = (cnt_ == 0), (cnt_ == nst - 1)
                        c.mm(Ob, Ob.t[0:64, :], vsc, vsc.t[:, kb, :], pb, pb.t[:, :], first, last)
                        c.mm(Lb, Lb.t[0:64, :], onesb, onesb.t[:, 0:64], pb, pb.t[:, :], first, last)
                        cnt_ += 1
                c.op("dve", lambda e: e.reciprocal(out=rl.t[0:64, :], in_=Lb.t[0:64, :]), [Lb], [rl])
                tm_ = tmpr.next()
                c.tt("dve", tm_, tm_.t[0:64, :], Ob, Ob.t[0:64, :], rl, rl.t[0:64, :], ALU.mult)
                for h in range(4):
                    c.tt("dve", mixT, mixT.t[0:64, 4 + h, qcols], tm_, tm_.t[0:64, h * 128:(h + 1) * 128], GS, GS.t[0:64, h, qcols], ALU.mult)

            xt = score.t[:, 0:4096].rearrange("p (c t) -> p c t", c=8)
            yv = score.t[:, 4096:8192].rearrange("p (c t) -> p c t", c=8)
            c.load(score, xt, xT[:, tok].rearrange("(c p) t -> p c t", p=128))
            Wo = score
            Wov = score.t[:, 8192:8192 + 6144].bitcast(BF16).rearrange("p (e d) -> p e d", e=12)
            for e_ in range(12):
                r0 = e_ * 128 if e_ < 4 else 512 + (e_ - 4) * 64
                nr = 128 if e_ < 4 else 64
                for hf in range(2):
                    st = tmpr.next()
                    c.load(st, st.t[0:nr, :], wout[r0:r0 + nr, hf * 512:(hf + 1) * 512])
                    c.cp("act" if hf else "dve", score, Wov[0:nr, e_, hf * 512:(hf + 1) * 512], st, st.t[0:nr, :])
            for dc in range(8):
                sb_ = Sr.next()
                dcs = slice(dc * 128, (dc + 1) * 128)
                for e_ in range(12):
                    if e_ < 4:
                        c.mm(sb_, sb_.t[:, :], Wo, Wov[:, e_, dcs], mixT, mixT.t[:, e_, :], e_ == 0, False)
                    else:
                        c.mm(sb_, sb_.t[:, :], Wo, Wov[0:64, e_, dcs], mixT, mixT.t[0:64, e_, :], False, e_ == 11)
                c.stt(score, yv[:, dc, :], score, xt[:, dc, :], ALPHA, sb_, sb_.t[:, :], ALU.mult, ALU.add)
                tm_ = tmpr.next()
                c.tt("pool", tm_, tm_.t[:, :], score, yv[:, dc, :], score, yv[:, dc, :], ALU.mult)
                c.mm(Ob, Ob.t[:, :], onesf, onesf.t[:, :], score, yv[:, dc, :], dc == 0, dc == 7)
                c.mm(Lb, Lb.t[:, :], onesf, onesf.t[:, :], tm_, tm_.t[:, :], dc == 0, dc == 7)
            c.ts("dve", n0, n0.t[:, :], Ob, Ob.t[:, :], 1.0 / 1024.0, None, ALU.mult)
            c.tt("dve", n1, n1.t[:, :], n0, n0.t[:, :], n0, n0.t[:, :], ALU.mult)
            c.stt(av, av.t[:, :], Lb, Lb.t[:, :], 1.0 / 1024.0, n1, n1.t[:, :], ALU.mult, ALU.subtract)
            c.ts("dve", av, av.t[:, :], av, av.t[:, :], LN_EPS, None, ALU.add)
            c.actf(av, av.t[:, :], av, av.t[:, :], AF.Ln)
            c.actf(av, av.t[:, :], av, av.t[:, :], AF.Exp, scale=-0.5)
            for dc in range(8):
                c.tt("dve", score, yv[:, dc, :], score, yv[:, dc, :], n0, n0.t[:, :], ALU.subtract)
                c.tt("pool", score, yv[:, dc, :], score, yv[:, dc, :], av, av.t[:, :], ALU.mult)
                c.ts("dve", score, yv[:, dc, :], score, yv[:, dc, :], lng.t[:, dc:dc + 1], lnb.t[:, dc:dc + 1], ALU.mult, ALU.add,
                     extra=[lng, lnb])
            outs.append(c.store(score, xo[:, tok].rearrange("(c p) t -> p c t", p=128), yv))
        c.p.emit(final_waits=outs)
    return nc


def _il_last(arrs, NT):
    a = np.stack([np.asarray(x).reshape(x.shape[:-1] + (NT, 512)) for x in arrs], axis=-2)
    return np.ascontiguousarray(a.reshape(a.shape[:-3] + (NT * 4 * 512,)))


def _il_rows(arrs, NT):
    a = np.stack([np.asarray(x).reshape(NT, 512, -1) for x in arrs], axis=1)
    return a.reshape(NT * 4 * 512, -1)


def make_masks(r):
    k = np.arange(128)[:, None, None, None]
    zt = np.arange(4)[None, :, None, None]
    kbl = np.arange(4)[None, None, :, None]
    q = np.arange(512)[None, None, None, :]
    mT = np.where(zt * 512 + kbl * 128 + k > r * 512 + q, NEG, 0.0).astype(np.float32).reshape(128, 16, 512)
    qq = np.arange(128)[:, None, None, None]
    qb = np.arange(4)[None, :, None, None]
    zt2 = np.arange(4)[None, None, :, None]
    kk = np.arange(512)[None, None, None, :]
    mA = np.where(zt2 * 512 + kk > r * 512 + qb * 128 + qq, NEG, 0.0).astype(np.float32).reshape(128, 16, 512)
    return mT.astype(NPBF), mA.astype(NPBF)


def run_B(ra, xTs, w_out_l, ln_g_l, ln_b_l, g_subln_l, lams, layer_idx, NT):
    nc = get_prog("B", NT)
    NCK = NT
    lambda_init = 0.8 - 0.6 * math.exp(-0.3 * layer_idx)
    lamv = np.ascontiguousarray(np.tile(np.concatenate(lams).astype(np.float32)[None, :], (128, 1)))
    lcon = np.ascontiguousarray(np.tile(np.array([[lambda_init, 1.0 - lambda_init]], np.float32), (128, 1)))
    lng = np.ascontiguousarray(ln_g_l.reshape(8, 128).T)
    lnb = np.ascontiguousarray(ln_b_l.reshape(8, 128).T)
    gsub = np.ascontiguousarray(g_subln_l.reshape(128, 1))
    ident = np.eye(128, dtype=np.float32).astype(NPBF)
    wout = np.ascontiguousarray(w_out_l)
    full = {}
    for b in range(2):
        cs = [ra[4 * b + r] for r in range(4)]
        f = {}
        f["kd_f"] = _il_last([x["kd"] for x in cs], NT)
        vd = _il_rows([x["vd"] for x in cs], NT)
        f["vd_f"] = np.ascontiguousarray(vd.reshape(NCK, 16, 128, 4, 128).transpose(3, 0, 2, 1, 4))
        f["kf_f"] = _il_last([x["kf"] for x in cs], NT)
        f["caug_f"] = _il_last([x["caug"] for x in cs], NT)
        vf = _il_rows([x["vf"] for x in cs], NT)
        f["vf_f"] = np.ascontiguousarray(vf.reshape(NCK, 16, 128, 4, 64).transpose(3, 0, 2, 1, 4))
        f["ks_f"] = _il_last([x["ks"] for x in cs], NT)
        vs = _il_rows([x["vs"] for x in cs], NT)
        f["vs_f"] = np.ascontiguousarray(vs.reshape(NCK, 16, 128, 64).transpose(0, 2, 1, 3))
        f["ki_f"] = _il_last([x["ki"] for x in cs], NT)
        clf = _il_last([x["cl"] for x in cs], NT)
        tot = clf.reshape(4, 4 * NT, 512)[:, :, 511]
        f["tot_f"] = np.ascontiguousarray(np.broadcast_to(tot[None], (128, 4, 4 * NT))).astype(np.float32)
        full[b] = f
    maps = []
    for cidx in range(8):
        b, r = cidx // 4, cidx % 4
        o = ra[cidx]
        mT, mA = make_masks(r)
        m = {"qd": o["qd"], "qf": o["qf"], "caug_o": o["caug"], "qs": o["qs"], "qi": o["qi"], "wi": o["wi"],
             "gd": o["gd"], "gs": o["gs"], "xT": xTs[cidx],
             "mr": np.ascontiguousarray(np.tile((np.arange(4) < r).astype(np.float32)[None, :], (128, 1))),
             "wout": wout, "lng": lng, "lnb": lnb, "gsub": gsub, "lamv": lamv, "lcon": lcon,
             "maskT": mT, "maskA": mA, "ident": ident}
        m.update(full[b])
        maps.append({k: np.ascontiguousarray(v) for k, v in m.items()})
    res = run_bass_kernel_spmd(nc, maps, core_ids=list(range(8)))
    return res.results


def forward(x, w_in, b_f, lam_q1, lam_k1, lam_q2, lam_k2, g_subln, w_out, ln_g, ln_b, NT, depth):
    xTs = shard_xT(np.asarray(x, np.float32), NT)
    for l in range(depth):
        ra = run_A(xTs, np.asarray(w_in[l], np.float32), np.asarray(b_f[l], np.float32), NT)
        rb = run_B(ra, xTs, np.asarray(w_out[l], np.float32), np.asarray(ln_g[l], np.float32),
                   np.asarray(ln_b[l], np.float32), np.asarray(g_subln[l], np.float32),
                   [np.asarray(lam_q1[l]), np.asarray(lam_k1[l]), np.asarray(lam_q2[l]), np.asarray(lam_k2[l])], l, NT)
        xTs = [np.asarray(rb[cidx]["xo"], np.float32) for cidx in range(8)]
    S = 4 * NT * 512
    out = np.empty((2, S, 1024), np.float32)
    for b in range(2):
        ov = out[b].reshape(NT, 4, 512, 1024)
        for r in range(4):
            ov[:, r] = xTs[4 * b + r].T.reshape(NT, 512, 1024)
    return out


def kernel(x, w_in, b_f, lam_q1, lam_k1, lam_q2, lam_k2, g_subln, w_out, ln_g, ln_b):
    return forward(x, w_in, b_f, lam_q1, lam_k1, lam_q2, lam_k2, g_subln, w_out, ln_g, ln_b, 8, DEPTH)
```
